# Optimizing a Trainium2 kernel written in Bass

```python
import jax, jax.numpy as jnp
from jax import lax
import numpy as np

D_MODEL = 1024
BATCH = 8
SEQ = 2048
DEPTH = 2

N_MIXERS = 2
HEAD_DIM = 64
HEADS_PER_GROUP = 5
DIL_WINDOWS = (128, 512, 2048)
DIL_RATES = (1, 4, 16)
N_GROUPS = len(DIL_RATES)
N_ATTN_HEADS = N_GROUPS * HEADS_PER_GROUP
ATTN_WIDTH = N_ATTN_HEADS * HEAD_DIM
BLOCK = 64
ROPE_THETA = 10000.0
CONV_WIDTH = 3
D_FF = 4 * D_MODEL
EPS = 1e-6
NEG_INF = -1e30

kernel_name = "hybrid_dilated_attn_shortconv_encoder"


def rms_norm(x, g):
    xf = x.astype(jnp.float32)
    y = xf * lax.rsqrt(jnp.mean(xf * xf, axis=-1, keepdims=True) + EPS)
    return (y * g.astype(jnp.float32)).astype(x.dtype)


def rope(x, pos):
    half = HEAD_DIM // 2
    inv = ROPE_THETA ** (-jnp.arange(half, dtype=jnp.float32) / half)
    ang = pos.astype(jnp.float32)[:, None] * inv[None, :]
    cos = jnp.cos(ang)[None, :, None, :]
    sin = jnp.sin(ang)[None, :, None, :]
    xf = x.astype(jnp.float32)
    x1, x2 = xf[..., :half], xf[..., half:]
    return jnp.concatenate([x1 * cos - x2 * sin, x2 * cos + x1 * sin], axis=-1).astype(x.dtype)


def dilated_window_group(q, k, v, dilation, radius):
    Bn, S, H, Dh = q.shape
    L = S // dilation
    nblk = -(-L // BLOCK)
    Lp = nblk * BLOCK

    def to_res(t):
        return t.reshape(Bn, L, dilation, H, Dh).transpose(0, 2, 1, 3, 4)

    qb = jnp.pad(to_res(q), ((0, 0), (0, 0), (0, Lp - L), (0, 0), (0, 0)))
    qb = qb.reshape(Bn, dilation, nblk, BLOCK, H, Dh)

    def kv_blocks(t):
        tp = jnp.pad(to_res(t), ((0, 0), (0, 0), (BLOCK, Lp - L + BLOCK), (0, 0), (0, 0)))
        tp = tp.reshape(Bn, dilation, nblk + 2, BLOCK, H, Dh)
        return jnp.concatenate([tp[:, :, :-2], tp[:, :, 1:-1], tp[:, :, 2:]], axis=3)

    kb = kv_blocks(k)
    vb = kv_blocks(v)

    n_idx = jnp.arange(nblk)[:, None, None]
    qi = n_idx * BLOCK + jnp.arange(BLOCK)[None, :, None]
    kj = n_idx * BLOCK - BLOCK + jnp.arange(3 * BLOCK)[None, None, :]
    valid = (kj >= 0) & (kj < L) & (jnp.abs(qi - kj) <= radius)

    s = jnp.einsum('brnqhe,brnkhe->brnhqk', qb.astype(jnp.float32), kb.astype(jnp.float32))
    s = s * (HEAD_DIM ** -0.5)
    s = jnp.where(valid[None, None, :, None, :, :], s, NEG_INF)
    m = jnp.max(s, axis=-1, keepdims=True)
    p = jnp.exp(s - m)
    den = jnp.sum(p, axis=-1, keepdims=True)
    o = jnp.einsum('brnhqk,brnkhe->brnqhe', p / den, vb.astype(jnp.float32))
    lse = (m + jnp.log(den))[..., 0]

    o = o.reshape(Bn, dilation, Lp, H, Dh)[:, :, :L]
    o = o.transpose(0, 2, 1, 3, 4).reshape(Bn, S, H, Dh)
    lse = lse.transpose(0, 1, 2, 4, 3).reshape(Bn, dilation, Lp, H)[:, :, :L]
    lse = lse.transpose(0, 2, 1, 3).reshape(Bn, S, H)
    return o.astype(q.dtype), lse


def dilated_attention_mixer(h, w_qkv, q_norm, k_norm, w_o):
    Bn, S, _ = h.shape
    pos = jnp.arange(S)
    qkv = h @ w_qkv
    q, k, v = jnp.split(qkv, 3, axis=-1)
    q = q.reshape(Bn, S, N_ATTN_HEADS, HEAD_DIM)
    k = k.reshape(Bn, S, N_ATTN_HEADS, HEAD_DIM)
    v = v.reshape(Bn, S, N_ATTN_HEADS, HEAD_DIM)
    q = rope(rms_norm(q, q_norm), pos)
    k = rope(rms_norm(k, k_norm), pos)
    outs, lses = [], []
    for g in range(N_GROUPS):
        sl = slice(g * HEADS_PER_GROUP, (g + 1) * HEADS_PER_GROUP)
        radius = DIL_WINDOWS[g] // (2 * DIL_RATES[g])
        o_g, lse_g = dilated_window_group(q[:, :, sl], k[:, :, sl], v[:, :, sl], DIL_RATES[g], radius)
        outs.append(o_g)
        lses.append(lse_g)
    wts = jax.nn.softmax(jnp.stack(lses, axis=0), axis=0)
    o = jnp.concatenate([outs[g] * wts[g][..., None].astype(outs[g].dtype) for g in range(N_GROUPS)], axis=2)
    return o.reshape(Bn, S, ATTN_WIDTH) @ w_o


def short_conv_mixer(h, w_in, conv_w, w_out):
    bcu = h @ w_in
    b, c, u = jnp.split(bcu, 3, axis=-1)
    u = c * u
    up = jnp.pad(u, ((0, 0), (1, 1), (0, 0)))
    y = up[:, :-2] * conv_w[0] + up[:, 1:-1] * conv_w[1] + up[:, 2:] * conv_w[2]
    return (b * y) @ w_out


def sq_relu_mlp(h, w_up, w_down):
    a = jax.nn.relu(h @ w_up)
    return (a * a) @ w_down


def setup_inputs(seed: int = 0) -> dict:
    key = jax.random.key(seed)
    ks = jax.random.split(key, 20)
    f32 = jnp.float32

    def dense(k, fi, fo):
        return jax.random.normal(k, (fi, fo), f32) * fi ** -0.5

    def gain(k, n):
        return 1.0 + 0.02 * jax.random.normal(k, (n,), f32)

    return {
        "x": jax.random.normal(ks[0], (BATCH, SEQ, D_MODEL), f32),
        "l0_norm_mix": gain(ks[1], D_MODEL),
        "l0_w_qkv": dense(ks[2], D_MODEL, 3 * ATTN_WIDTH),
        "l0_q_norm": gain(ks[3], HEAD_DIM),
        "l0_k_norm": gain(ks[4], HEAD_DIM),
        "l0_w_o": dense(ks[5], ATTN_WIDTH, D_MODEL),
        "l0_norm_mlp": gain(ks[6], D_MODEL),
        "l0_w_up": dense(ks[7], D_MODEL, D_FF),
        "l0_w_down": dense(ks[8], D_FF, D_MODEL),
        "l1_norm_mix": gain(ks[9], D_MODEL),
        "l1_w_in": dense(ks[10], D_MODEL, 3 * D_MODEL),
        "l1_conv_w": jax.random.normal(ks[11], (CONV_WIDTH, D_MODEL), f32) * CONV_WIDTH ** -0.5,
        "l1_w_out": dense(ks[12], D_MODEL, D_MODEL),
        "l1_norm_mlp": gain(ks[13], D_MODEL),
        "l1_w_up": dense(ks[14], D_MODEL, D_FF),
        "l1_w_down": dense(ks[15], D_FF, D_MODEL),
        "final_norm": gain(ks[16], D_MODEL),
    }


def reference(x, l0_norm_mix, l0_w_qkv, l0_q_norm, l0_k_norm, l0_w_o, l0_norm_mlp, l0_w_up, l0_w_down,
              l1_norm_mix, l1_w_in, l1_conv_w, l1_w_out, l1_norm_mlp, l1_w_up, l1_w_down, final_norm):
    norm_mix = [l0_norm_mix, l1_norm_mix]
    norm_mlp = [l0_norm_mlp, l1_norm_mlp]
    mixer_params = [(l0_w_qkv, l0_q_norm, l0_k_norm, l0_w_o), (l1_w_in, l1_conv_w, l1_w_out)]
    mlp_params = [(l0_w_up, l0_w_down), (l1_w_up, l1_w_down)]
    for i in range(DEPTH):
        h = rms_norm(x, norm_mix[i])
        if i % N_MIXERS == 0:
            x = x + dilated_attention_mixer(h, *mixer_params[i])
        else:
            x = x + short_conv_mixer(h, *mixer_params[i])
        x = x + sq_relu_mlp(rms_norm(x, norm_mlp[i]), *mlp_params[i])
    return rms_norm(x, final_norm)
```

```python
from contextlib import ExitStack

import numpy as np
import concourse.bass as bass
import concourse.mybir as mybir
from concourse.bass_utils import run_bass_kernel_spmd

F32 = mybir.dt.float32
BF16 = mybir.dt.bfloat16
AF = mybir.ActivationFunctionType
ALU = mybir.AluOpType

T = 2048
EPS = 1e-6
DILS = (1, 4, 16)
C_COS, C_SIN, C_PM, C_BO, C_ON, C_M4, C_O64, NCST = 0, 2048, 4096, 4224, 4352, 4480, 4992, 5056
S_GN, S_GQK, S_CW, S_EPS, NSM = 0, 40, 42, 66, 68
AR_UNITS = 33500


class _Op:
    __slots__ = ("eng", "fn", "chan", "fill", "deps", "sig", "sig_idx", "idx")


class Prog:
    ENGS = ("pe", "act", "dve", "pool", "sp")

    def __init__(self):
        self.ops = []
        self.res = {}
        self.chan_count = {}
        self.fill_end = {}
        self.chan_eng = {}
        self.final_chans = []

    def alias(self, new, olds):
        st = self.res.setdefault(new, [None, {}])
        for o in olds:
            so = self.res.get(o)
            if so is None or o == new:
                continue
            if so[0] is not None:
                st[1][("w", o)] = so[0]
            for k, op in list(so[1].items()):
                st[1][(k, o)] = op

    def op(self, eng, fn, reads=(), writes=(), chan=None, fill=None):
        o = _Op()
        o.eng, o.fn, o.chan, o.fill = eng, fn, chan, fill
        o.idx = len(self.ops)
        o.sig = False
        o.sig_idx = 0
        deps = set()
        for r in reads:
            st = self.res.get(r)
            if st is not None and st[0] is not None:
                deps.add(st[0])
            if st is not None and r.startswith("ps") and eng in ("act", "dve"):
                other = "dve" if eng == "act" else "act"
                if other in st[1]:
                    deps.add(st[1][other])
        for w in writes:
            st = self.res.get(w)
            if st is not None:
                if st[0] is not None:
                    deps.add(st[0])
                deps.update(st[1].values())
        key = eng if chan is None else ("dma", chan)
        for r in reads:
            st = self.res.setdefault(r, [None, {}])
            st[1][key] = o
        for w in writes:
            self.res[w] = [o, {}]
        deps.discard(o)
        if chan is not None:
            deps = {d for d in deps if not (d.chan == chan and d.fill == fill)}
        o.deps = deps
        if chan is not None:
            c = self.chan_count.get(chan, 0) + 1
            self.chan_count[chan] = c
            self.fill_end[(chan, fill)] = c
            assert self.chan_eng.setdefault(chan, eng) == eng
        self.ops.append(o)
        return o

    def emit(self, nc, stack):
        for o in self.ops:
            for d in o.deps:
                if d.chan is None and not (d.eng == "pe" and o.eng == "pe"):
                    d.sig = True
        cnt = {e: 0 for e in self.ENGS}
        for o in self.ops:
            if o.chan is None and o.sig:
                cnt[o.eng] += 1
                o.sig_idx = cnt[o.eng]
        sems = {}
        for e in self.ENGS:
            sems[("eng", e)] = stack.enter_context(nc.semaphore("s_" + e))
        for c in self.chan_count:
            sems[("chan", c)] = stack.enter_context(nc.semaphore("c_" + c))
        block = stack.enter_context(nc.Block())
        prog = self

        def run_stream(ename, eng):
            waited = {}
            for o in prog.ops:
                if o.eng != ename:
                    continue
                needs = {}
                for d in o.deps:
                    if d.chan is not None:
                        k = ("chan", d.chan)
                        v = 16 * prog.fill_end[(d.chan, d.fill)]
                    else:
                        if d.eng == ename and ename == "pe":
                            continue
                        k = ("eng", d.eng)
                        v = d.sig_idx
                    if v > needs.get(k, 0):
                        needs[k] = v
                for k, v in needs.items():
                    if waited.get(k, 0) >= v:
                        continue
                    eng.wait_ge(sems[k], v)
                    waited[k] = v
                inst = o.fn(eng)
                if o.chan is not None:
                    inst.then_inc(sems[("chan", o.chan)], 16)
                elif o.sig:
                    inst.then_inc(sems[("eng", ename)], 1)
            for c, n in prog.chan_count.items():
                if prog.chan_eng[c] == ename and waited.get(("chan", c), 0) < 16 * n:
                    eng.wait_ge(sems[("chan", c)], 16 * n)

        block.tensor(lambda e: run_stream("pe", e))
        block.scalar(lambda e: run_stream("act", e))
        block.vector(lambda e: run_stream("dve", e))
        block.gpsimd(lambda e: run_stream("pool", e))
        block.sync(lambda e: run_stream("sp", e))


class _Stop(Exception):
    pass


class _Ring:
    def __init__(self, ids):
        self.ids = list(ids)
        self.i = 0

    def next(self):
        b = self.ids[self.i % len(self.ids)]
        self.i += 1
        return b


def build_nc(stop_after=None):
    nc = bass.Bass("TRN2", target_bir_lowering=False)

    def din(name, shape):
        return nc.dram_tensor(name, list(shape), F32, kind="ExternalInput").ap()

    xT_d = din("xT", [1024, T])
    cst_d = din("cst", [128, NCST])
    sm_d = din("sm", [128, NSM])
    wqkv_d = din("wqkv", [9, 1024, 384])
    wo_d = din("wo", [9, 128, 1024])
    wup_d = [din("wup0", [1024, 4096]), din("wup1", [1024, 4096])]
    wdn_d = [din("wdn0", [4096, 1024]), din("wdn1", [4096, 1024])]
    win_d = din("win", [8, 1024, 384])
    wout_d = din("wout", [1024, 1024])
    out_d = nc.dram_tensor("outT", [1024, T], F32, kind="ExternalOutput").ap()

    with ExitStack() as st:
        def sb(name, shape, dt):
            return st.enter_context(nc.sbuf_tensor(name, list(shape), dt))

        X = sb("X", [128, 8, T], F32)
        hT = sb("hT", [128, 8, T], BF16)
        cst = sb("cstb", [128, NCST], BF16)
        sm = sb("smf", [128, NSM], F32)
        wsl = [sb(f"wsl{i}", [128, 4096], BF16) for i in range(4)]
        nsq = [sb(f"nsq{i}", [128, 512], BF16) for i in range(2)]
        nrs = sb("nrs", [128, 1024], BF16)
        nrstd = nrs[:].bitcast(F32)
        nrb = [nrs[:, 0:512], nrs[:, 512:1024]]
        NR = ["nr0", "nr1"]
        arena = sb("arena", [128, AR_UNITS], BF16)
        ps = [st.enter_context(nc.psum_tensor(f"ps{i}", [128, 512], F32)) for i in range(8)]

        p = Prog()

        def MM(out, lhsT, rhs, start, stop, R, W):
            p.op("pe", lambda e: e.matmul(out, lhsT=lhsT, rhs=rhs, start=start, stop=stop), R, W)

        def ACT(out, in_, func, R, W, scale=None, bias=None):
            kw = {}
            if scale is not None:
                kw["scale"] = scale
            if bias is not None:
                kw["bias"] = bias
            p.op("act", lambda e: e.activation(out=out, in_=in_, func=func, **kw), R, W)

        def TT(eng, out, in0, in1, op, R, W):
            p.op(eng, lambda e: e.tensor_tensor(out=out, in0=in0, in1=in1, op=op), R, W)

        def TS(eng, out, in0, s1, op0, R, W):
            p.op(eng, lambda e: e.tensor_scalar(out=out, in0=in0, scalar1=s1, scalar2=None, op0=op0), R, W)

        def STT(out, in0, scalar, in1, op0, op1, R, W):
            p.op("dve", lambda e: e.scalar_tensor_tensor(out=out, in0=in0, scalar=scalar, in1=in1,
                                                         op0=op0, op1=op1), R, W)

        def CP(eng, out, in_, R, W):
            if eng == "act":
                p.op("act", lambda e: e.copy(out=out, in_=in_), R, W)
            else:
                p.op(eng, lambda e: e.tensor_copy(out=out, in_=in_), R, W)

        def MEMSET(eng, ap, val, W):
            p.op(eng, lambda e: e.memset(ap, val), (), W)

        def DMA(eng, out, in_, R, W, chan, fill):
            p.op(eng, lambda e: e.dma_start(out=out, in_=in_), R, W, chan=chan, fill=fill)

        arena_bufs = []

        def aalloc(phase, units, res_names, cursor):
            s = cursor[0]
            e = s + units + (units & 1)
            assert e <= AR_UNITS, (phase, e)
            cursor[0] = e
            for (ph, s0, e0, names) in arena_bufs:
                if ph != phase and s0 < e and s < e0:
                    for rn in res_names:
                        p.alias(rn, names)
            arena_bufs.append((phase, s, e, list(res_names)))
            return arena[:, s:s + units]

        def as_f32(ap):
            return ap.bitcast(F32)

        cosT = cst[:, C_COS:C_COS + T]
        sinS = cst[:, C_SIN:C_SIN + T]
        Pm = cst[:, C_PM:C_PM + 128]
        blockones = cst[:, C_BO:C_BO + 128]
        onesN = cst[:, C_ON:C_ON + 128]
        mask4 = cst[:, C_M4:C_M4 + 512].rearrange("p (s i) -> p s i", s=4)
        ones64 = cst[:, C_O64:C_O64 + 64]
        eps_ap = sm[:, S_EPS:S_EPS + 1]

        pn = _Ring([3, 4])
        rings = {"pp": _Ring([0, 1, 2, 5, 6, 7])}

        def PSB(b):
            return "ps%d" % b

        wsteps = []

        def add_wstep(fn):
            wsteps.append(fn)
            return len(wsteps) - 1

        wstate = {"next": 0}

        def prefetch(upto):
            while wstate["next"] <= min(upto, len(wsteps) - 1):
                k = wstate["next"]
                slot = k % 4
                gate = ["h7_0"] if 1 <= k <= 3 else ()
                for (o_ap, i_ap) in wsteps[k](wsl[slot]):
                    DMA("pool", o_ap, i_ap, gate, ["w%d" % slot], chan="w%d" % slot, fill=k)
                wstate["next"] += 1

        def wslot(k, pf_to=None):
            prefetch(k + 3 if pf_to is None else pf_to)
            return k % 4, wsl[k % 4], "w%d" % (k % 4)

        def mk_rows(dview, kc, cols):
            def fn(tile):
                o = tile[:, 0:kc * cols].rearrange("p (c f) -> p c f", c=kc)
                return [(o, dview.rearrange("(c p) f -> p c f", p=128))]
            return fn

        ws_qkv = [add_wstep(mk_rows(wqkv_d[u], 8, 384)) for u in range(9)]

        def mk_wo(k):
            def fn(tile):
                o = tile[:, 0:3 * 1024].rearrange("p (c f) -> p c f", c=3)
                return [(o, wo_d[3 * k:3 * k + 3].rearrange("u p f -> p u f"))]
            return fn

        ws_wo = [add_wstep(mk_wo(k)) for k in range(3)]

        def mlp_steps(l):
            r = []
            for fg in range(8):
                a = add_wstep(mk_rows(wup_d[l][:, fg * 512:(fg + 1) * 512], 8, 512))
                b = add_wstep(mk_rows(wdn_d[l][fg * 512:(fg + 1) * 512, :], 4, 1024))
                r.append((a, b))
            return r

        ws_mlp0 = mlp_steps(0)
        ws_win = [add_wstep(mk_rows(win_d[i], 8, 384)) for i in range(8)]
        ws_wout = [add_wstep(mk_rows(wout_d[k * 512:(k + 1) * 512, :], 4, 1024)) for k in range(2)]
        ws_mlp1 = mlp_steps(1)

        DMA("sp", sm[:], sm_d, (), ["sm"], chan="sm", fill=0)
        dbg = stop_after or ""
        for j in range((NCST + 1023) // 1024):
            if dbg.startswith("load") and "c" not in dbg[4:]:
                break
            c0, c1 = j * 1024, min(NCST, (j + 1) * 1024)
            DMA("pool", cst[:, c0:c1], cst_d[:, c0:c1], (), ["cst"], chan="cst", fill=0)
        xv = xT_d.rearrange("(c p) t -> p c t", p=128)
        for tt in range(4):
            for c in range(8):
                DMA("sp", X[:, c, tt * 512:(tt + 1) * 512], xv[:, c, tt * 512:(tt + 1) * 512],
                    ["xg%d" % (tt - 1)] if tt else (), ["X%d_%d" % (c, tt), "xg%d" % tt],
                    chan="x%d" % tt, fill=0)
        if not (dbg.startswith("load") and "w" not in dbg[4:]):
            prefetch(0)

        def XR(c, tt):
            return "X%d_%d" % (c, tt)

        def HR(c, tt):
            return "h%d_%d" % (c, tt)

        def cols(tt):
            return slice(tt * 512, (tt + 1) * 512)

        def norm_stage(gidx, out_fn):
            for tt in range(4):
                norm_tile(gidx, tt, out_fn)

        def norm_tile(gidx, tt, out_fn):
            if True:
                b = pn.next()
                for c in range(8):
                    sq = nsq[c % 2]
                    ACT(sq[:], X[:, c, cols(tt)], AF.Square, [XR(c, tt)], ["nsq%d" % (c % 2)])
                    MM(ps[b][:], onesN, sq[:], c == 0, c == 7, ["cst", "nsq%d" % (c % 2)], [PSB(b)])
                ACT(nrstd[:], ps[b][:], AF.Ln, [PSB(b), "sm"], NR, bias=eps_ap)
                ACT(nrstd[:], nrstd[:], AF.Exp, NR, NR, scale=-0.5)
                for c in range(8):
                    out_fn(gidx, c, tt)

        def tail_norm(tt, fn):
            if fn is None:
                return
            if tt >= 1:
                fn(tt - 1)
            if tt == 3:
                fn(3)

        def norm_to_hT(gidx, c, tt):
            g_ap = sm[:, S_GN + gidx * 8 + c:S_GN + gidx * 8 + c + 1]
            if gidx > 0 and c in (1, 3, 5):
                j = (c // 2) % 2
                ACT(nsq[j][:], X[:, c, cols(tt)], AF.Copy, [XR(c, tt), "sm"], ["nsq%d" % j], scale=g_ap)
                TT("pool", hT[:, c, cols(tt)], nsq[j][:], nrstd[:], ALU.mult, ["nsq%d" % j] + NR, [HR(c, tt)])
                return
            STT(hT[:, c, cols(tt)], X[:, c, cols(tt)], g_ap,
                nrstd[:], ALU.mult, ALU.mult, [XR(c, tt), "sm"] + NR, [HR(c, tt)])

        def dump_X():
            ov = out_d.rearrange("(c p) t -> p c t", p=128)
            for c in range(8):
                DMA("sp", ov[:, c, :], X[:, c, :], [XR(c, tt) for tt in range(4)], (), chan="o0", fill=0)
            p.final_chans.append("o0")

        def finish():
            p.emit(nc, st)
            return nc

        if dbg.startswith("load"):
            dump_X()
            return finish()
        norm_stage(0, norm_to_hT)
        if stop_after == "norm0":
            dump_X()
            return finish()

        cur = [0]
        qT = aalloc("A", 2048, ["qT"], cur)
        kT = aalloc("A", 2048, ["kT"], cur)
        Vt = aalloc("A", 2048, ["Vt"], cur).rearrange("p (k f) -> p k f", k=16)
        oT = [aalloc("A", 2048, ["oT%d" % u], cur) for u in range(9)]
        dtot = as_f32(aalloc("A", 4096, ["dtot"], cur))
        qg = [aalloc("A", 512, ["qg%d" % i], cur) for i in range(2)]
        T1 = [aalloc("A", 512, ["t1_%d" % i], cur) for i in range(2)]
        T2 = [aalloc("A", 512, ["t2_%d" % i], cur) for i in range(2)]
        Pb = [aalloc("A", 512, ["P%d" % i], cur).rearrange("p (s i) -> p s i", s=4) for i in range(3)]

        for u_ in (6, 7, 8):
            MEMSET("pool", oT[u_][64:128, :], 0.0, ["oT%d" % u_])
        rings["pp"] = _Ring([0, 1])
        pSA = _Ring([0, 1])
        pSB = _Ring([3, 4])
        pOo = _Ring([2, 5])
        pOd = _Ring([6, 7])
        qk_i = [0]
        p_i = [0]

        def attn_unit(u, pi, g):
            D = DILS[g]
            L = T // D
            nb = L // 128
            slot, wt, wres = wslot(ws_qkv[u])
            W = wt[:, 0:8 * 384].rearrange("p (c f) -> p c f", c=8)
            pp = rings["pp"]
            its = [(which, dst, dres, coff, tt) for which, (dst, dres, coff) in
                   enumerate([(qT, "qT", 0), (kT, "kT", 128)]) for tt in range(4)]

            def proj(it):
                which, dst, dres, coff, tt = it
                b = pp.next()
                for kc in range(8):
                    MM(ps[b][:], W[:, kc, coff:coff + 128], hT[:, kc, cols(tt)], kc == 0, kc == 7,
                       [wres, HR(kc, tt)], [PSB(b)])
                return b

            nxt = proj(its[0])
            for k_, it in enumerate(its):
                if True:
                    which, dst, dres, coff, tt = it
                    b = nxt
                    nxt = proj(its[k_ + 1]) if k_ + 1 < len(its) else None
                    i = qk_i[0]
                    qk_i[0] += 1
                    sq, sqr = nsq[i % 2], "nsq%d" % (i % 2)
                    qgb, qgr = qg[i % 2], "qg%d" % (i % 2)
                    ACT(sq[:], ps[b][:], AF.Square, [PSB(b)], [sqr])
                    ACT(qgb, ps[b][:], AF.Copy, [PSB(b), "sm"], [qgr],
                        scale=sm[:, S_GQK + which:S_GQK + which + 1])
                    bs = pn.next()
                    MM(ps[bs][:], blockones, sq[:], True, True, ["cst", sqr], [PSB(bs)])
                    bw = pn.next()
                    MM(ps[bw][:], Pm, qgb, True, True, ["cst", qgr], [PSB(bw)])
                    j = i % 2
                    ACT(ps[bs][:], ps[bs][:], AF.Ln, [PSB(bs), "sm"], [PSB(bs)], bias=eps_ap)
                    ACT(nrb[j], ps[bs][:], AF.Exp, [PSB(bs)], ["nr%d" % j], scale=-0.5)
                    t1, t1r = T1[j], "t1_%d" % j
                    t2, t2r = T2[j], "t2_%d" % j
                    TT("pool", t1, qgb, cosT[:, cols(tt)], ALU.mult, [qgr, "cst"], [t1r])
                    TT("dve", t2, ps[bw][:], sinS[:, cols(tt)], ALU.mult, [PSB(bw), "cst"], [t2r])
                    TT("dve", t1, t1, t2, ALU.add, [t1r, t2r], [t1r])
                    w = 512 // D
                    dv = dst.rearrange("p (r j) -> p r j", r=D)[:, :, tt * w:(tt + 1) * w]
                    s1 = t1.rearrange("p (j r) -> p r j", r=D)
                    s2 = nrb[j].rearrange("p (j r) -> p r j", r=D)
                    TT("dve", dv, s1, s2, ALU.mult, [t1r, "nr%d" % j], [dres])
            if dbg == "u%dqk" % u:
                CP("dve", X[:, 0, :], qT, ["qT"], [XR(0, t_) for t_ in range(4)])
                CP("dve", X[:, 1, :], kT, ["kT"], [XR(1, t_) for t_ in range(4)])
                raise _Stop()
            for k4 in range(4):
                b = pp.next()
                for q4 in range(4):
                    kt = k4 * 4 + q4
                    r, bb = kt // nb, kt % nb
                    t0 = 128 * bb * D + r
                    tts = sorted(set([t0 // 512, (t0 + 127 * D) // 512]))
                    if D == 16:
                        tts = [0, 1, 2, 3]
                    for kc in range(8):
                        lhsT = hT[:, kc, t0:t0 + 127 * D + 1:D] if D > 1 else hT[:, kc, t0:t0 + 128]
                        MM(ps[b][:, q4 * 128:(q4 + 1) * 128], lhsT, W[:, kc, 256:384], kc == 0, kc == 7,
                           [wres] + [HR(kc, x) for x in tts], [PSB(b)])
                CP("act", Vt[:, k4 * 4:(k4 + 1) * 4, :], ps[b][:].rearrange("p (k f) -> p k f", k=4),
                   [PSB(b)], ["Vt"])
            if dbg == "u%dv" % u:
                CP("dve", X[:, 2, :], Vt.rearrange("p k f -> p (k f)"), ["Vt"], [XR(2, t_) for t_ in range(4)])
                raise _Stop()
            blocks = [(r, a) for r in range(D) for a in range(nb + 1)]
            heads = (0,) if pi == 2 else (0, 1)
            nh = len(heads)

            def geom(blk):
                r, a = blk
                i0, i1 = (64, 128) if a == 0 else ((0, 64) if a == nb else (0, 128))
                tiles = [bt for bt in (a - 1, a) if 0 <= bt < nb]
                return r, a, i0, i1, tiles

            def slot_of(h, bt, a):
                return 2 * h + (0 if bt == a - 1 else 1)

            def emit_S(blk):
                r, a, i0, i1, tiles = geom(blk)
                nq = i1 - i0
                jlo = 128 * a - 64 + i0
                banks = (pSA.next(), pSB.next() if nh == 2 else None)
                for h in heads:
                    for bt in tiles:
                        s = 0 if bt == a - 1 else 1
                        MM(ps[banks[h]][:, s * 128:s * 128 + nq],
                           kT[h * 64:(h + 1) * 64, r * L + 128 * bt:r * L + 128 * bt + 128],
                           qT[h * 64:(h + 1) * 64, r * L + jlo:r * L + jlo + nq], True, True,
                           ["qT", "kT"], [PSB(banks[h])])
                return banks

            def emit_P(blk, banks):
                r, a, i0, i1, tiles = geom(blk)
                nq = i1 - i0
                pi_ = p_i[0] % 3
                p_i[0] += 1
                P, pr = Pb[pi_], "P%d" % pi_
                for h in heads:
                    psv = ps[banks[h]][:, 0:256].rearrange("p (s i) -> p s i", s=2)
                    if len(tiles) == 2:
                        ACT(P[:, 2 * h:2 * h + 2, :], psv, AF.Exp, [PSB(banks[h])], [pr], scale=0.125)
                    else:
                        s0 = 1 if a == 0 else 0
                        ACT(P[:, 2 * h + s0, 0:nq], psv[:, s0, 0:nq], AF.Exp, [PSB(banks[h])], [pr], scale=0.125)
                if len(tiles) == 2:
                    pv, mv = P[:, 0:2 * nh, :], mask4[:, 0:2 * nh, :]
                else:
                    s0 = 1 if a == 0 else 0
                    pv = P[:, s0:2 * nh:2, 0:nq]
                    mv = mask4[:, s0:2 * nh:2, i0:i1]
                TT("dve", pv, pv, mv, ALU.mult, [pr, "cst"], [pr])
                return P, pr

            def emit_O(blk, P, pr):
                r, a, i0, i1, tiles = geom(blk)
                nq = i1 - i0
                jlo = 128 * a - 64 + i0
                bo, bd = pOo.next(), pOd.next()
                for kind, bank in ((0, bo), (1, bd)):
                    for h in heads:
                        hp = slice(h * 64, (h + 1) * 64)
                        for ti, bt in enumerate(tiles):
                            s = slot_of(h, bt, a)
                            kt = r * nb + bt
                            lhsT = Vt[:, kt, hp] if kind == 0 else ones64
                            MM(ps[bank][hp, 0:nq], lhsT, P[:, s, 0:nq],
                               ti == 0, ti == len(tiles) - 1,
                               [pr, "Vt" if kind == 0 else "cst"], [PSB(bank)])
                t0 = jlo * D + r
                tok = slice(t0, t0 + (nq - 1) * D + 1, D) if D > 1 else slice(t0, t0 + nq)
                rows = slice(0, 64 * nh)
                CP("act", oT[u][rows, tok], ps[bo][rows, 0:nq], [PSB(bo)], ["oT%d" % u])
                if g == 0:
                    CP("dve", dtot[rows, tok], ps[bd][rows, 0:nq], [PSB(bd)], ["dtot"])
                else:
                    TT("dve", dtot[rows, tok], dtot[rows, tok], ps[bd][rows, 0:nq], ALU.add,
                       [PSB(bd), "dtot"], ["dtot"])

            n = len(blocks)
            sb_ = {0: emit_S(blocks[0])}
            if n > 1:
                sb_[1] = emit_S(blocks[1])
            Pcur = emit_P(blocks[0], sb_[0])
            for bi in range(n):
                if bi + 2 < n:
                    sb_[bi + 2] = emit_S(blocks[bi + 2])
                Pnext = emit_P(blocks[bi + 1], sb_[bi + 1]) if bi + 1 < n else None
                emit_O(blocks[bi], *Pcur)
                Pcur = Pnext

        for pi in range(3):
            for g in range(3):
                try:
                    attn_unit(pi * 3 + g, pi, g)
                except _Stop:
                    dump_X()
                    return finish()
                if dbg == "u%datt" % (pi * 3 + g):
                    CP("dve", X[:, 3, :], oT[pi * 3 + g], ["oT%d" % (pi * 3 + g)], [XR(3, t_) for t_ in range(4)])
                    CP("dve", X[:, 4, :], dtot, ["dtot"], [XR(4, t_) for t_ in range(4)])
                    dump_X()
                    return finish()
            ACT(dtot, dtot, AF.Ln, ["dtot"], ["dtot"])
            ACT(dtot, dtot, AF.Exp, ["dtot"], ["dtot"], scale=-1.0)
            for g in range(3):
                u = pi * 3 + g
                TT("pool", oT[u], oT[u], dtot, ALU.mult, ["oT%d" % u, "dtot"], ["oT%d" % u])

        rings["pp"] = _Ring([0, 1, 2, 5, 6, 7])
        wo_slots = [wslot(k, pf_to=ws_wo[0] + 3) for k in ws_wo]
        for tt in range(4):
            for oc in range(8):
                b = rings["pp"].next()
                for u in range(9):
                    _, wt, wres = wo_slots[u // 3]
                    Wv = wt[:, 0:3 * 1024].rearrange("p (c f) -> p c f", c=3)
                    MM(ps[b][:], Wv[:, u % 3, oc * 128:(oc + 1) * 128], oT[u][:, cols(tt)], u == 0, u == 8,
                       [wres, "oT%d" % u], [PSB(b)])
                TT("dve", X[:, oc, cols(tt)], X[:, oc, cols(tt)], ps[b][:], ALU.add,
                   [XR(oc, tt), PSB(b)], [XR(oc, tt)])
            tail_norm(tt, lambda t_: norm_tile(1, t_, norm_to_hT))
        if stop_after == "attn":
            dump_X()
            return finish()

        mlp_count = [0]

        def mlp(steps, tail_fn_factory):
            ph = "M%d" % mlp_count[0]
            mlp_count[0] += 1
            cur = [0]
            aTb = [aalloc(ph, 8192, ["a%d_%d_%d" % (bf, fc, tt) for fc in range(4) for tt in range(4)], cur)
                   .rearrange("p (c t) -> p c t", c=4) for bf in range(2)]
            rb = [aalloc(ph, 512, ["r%d" % i], cur) for i in range(3)]
            tail_fn = tail_fn_factory(ph, cur)
            pp = rings["pp"]
            ri = 0
            for fg in range(8):
                su, sd = steps[fg]
                _, wtu, wru = wslot(su)
                _, wtd, wrd = wslot(sd, pf_to=su + 3)
                WU = wtu[:, 0:8 * 512].rearrange("p (c f) -> p c f", c=8)
                WD = wtd[:, 0:4 * 1024].rearrange("p (c f) -> p c f", c=4)
                bf = fg % 2
                aT = aTb[bf]
                for tt in range(4):
                    for fc in range(4):
                        b = pp.next()
                        for kc in range(8):
                            MM(ps[b][:], WU[:, kc, fc * 128:(fc + 1) * 128], hT[:, kc, cols(tt)],
                               kc == 0, kc == 7, [wru, HR(kc, tt)], [PSB(b)])
                        r, rr = rb[ri % 3], "r%d" % (ri % 3)
                        ri += 1
                        ACT(r, ps[b][:], AF.Relu, [PSB(b)], [rr])
                        TT("pool", aT[:, fc, cols(tt)], r, r, ALU.mult, [rr], ["a%d_%d_%d" % (bf, fc, tt)])
                for tt in range(4):
                    for oc in range(8):
                        b = pp.next()
                        for fc in range(4):
                            MM(ps[b][:], WD[:, fc, oc * 128:(oc + 1) * 128], aT[:, fc, cols(tt)],
                               fc == 0, fc == 3, [wrd, "a%d_%d_%d" % (bf, fc, tt)], [PSB(b)])
                        TT("dve", X[:, oc, cols(tt)], X[:, oc, cols(tt)], ps[b][:], ALU.add,
                           [XR(oc, tt), PSB(b)], [XR(oc, tt)])
                    if fg == 7:
                        tail_norm(tt, tail_fn)

        mlp(ws_mlp0, lambda ph, cur: (lambda t_: norm_tile(2, t_, norm_to_hT)))
        if stop_after == "mlp0":
            dump_X()
            return finish()

        cur = [0]
        zT = aalloc("C", 8 * T, ["z%d" % i for i in range(8)], cur).rearrange("p (c t) -> p c t", c=8)
        vb = [as_f32(aalloc("C", 2 * (T + 2), ["v0"], cur))] * 2
        yb = as_f32(aalloc("C", 2 * T, ["y"], cur))
        bsb = [aalloc("C", T, ["b%d" % i], cur) for i in range(2)]
        csb = [as_f32(aalloc("C", 1024, ["c%d" % i], cur)) for i in range(2)]
        MEMSET("pool", vb[0][:, 0:1], 0.0, ["v0"])
        MEMSET("pool", vb[0][:, T + 1:T + 2], 0.0, ["v0"])
        pp = rings["pp"]
        ci = 0
        for i in range(8):
            _, wt, wres = wslot(ws_win[i])
            W = wt[:, 0:8 * 384].rearrange("p (c f) -> p c f", c=8)
            v, vr = vb[0], "v0"
            bs_, br = bsb[i % 2], "b%d" % (i % 2)
            for tt in range(4):
                banks = []
                for coff in (128, 256, 0):
                    b = pp.next()
                    banks.append(b)
                    for kc in range(8):
                        MM(ps[b][:], W[:, kc, coff:coff + 128], hT[:, kc, cols(tt)], kc == 0, kc == 7,
                           [wres, HR(kc, tt)], [PSB(b)])
                cs, cr = csb[ci % 2], "c%d" % (ci % 2)
                ci += 1
                CP("act", cs, ps[banks[0]][:], [PSB(banks[0])], [cr])
                TT("dve", v[:, 1 + tt * 512:1 + (tt + 1) * 512], cs, ps[banks[1]][:], ALU.mult,
                   [cr, PSB(banks[1])], [vr])
                CP("act", bs_[:, cols(tt)], ps[banks[2]][:], [PSB(banks[2])], [br])
            cw = lambda k: sm[:, S_CW + k * 8 + i:S_CW + k * 8 + i + 1]
            ACT(yb, v[:, 0:T], AF.Copy, [vr, "sm"], ["y"], scale=cw(0))
            STT(yb, v[:, 1:T + 1], cw(1), yb, ALU.mult, ALU.add, [vr, "sm", "y"], ["y"])
            STT(yb, v[:, 2:T + 2], cw(2), yb, ALU.mult, ALU.add, [vr, "sm", "y"], ["y"])
            TT("pool", zT[:, i, :], bs_, yb, ALU.mult, [br, "y"], ["z%d" % i])
        wout_slots = [wslot(k, pf_to=ws_wout[0] + 3) for k in ws_wout]
        for tt in range(4):
            for oc in range(8):
                b = pp.next()
                for kc in range(8):
                    _, wt, wres = wout_slots[kc // 4]
                    Wv = wt[:, 0:4 * 1024].rearrange("p (c f) -> p c f", c=4)
                    MM(ps[b][:], Wv[:, kc % 4, oc * 128:(oc + 1) * 128], zT[:, kc, cols(tt)], kc == 0, kc == 7,
                       [wres, "z%d" % kc], [PSB(b)])
                TT("dve", X[:, oc, cols(tt)], X[:, oc, cols(tt)], ps[b][:], ALU.add,
                   [XR(oc, tt), PSB(b)], [XR(oc, tt)])
            tail_norm(tt, lambda t_: norm_tile(3, t_, norm_to_hT))
        if stop_after == "conv":
            dump_X()
            return finish()

        ov = out_d.rearrange("(c p) t -> p c t", p=128)

        def final_factory(ph, cur):
            ost = as_f32(aalloc(ph, 8192, ["ost"], cur)).rearrange("p (c t) -> p c t", c=8)

            def norm_to_out(gidx, c, tt):
                STT(ost[:, c, :], X[:, c, cols(tt)], sm[:, S_GN + gidx * 8 + c:S_GN + gidx * 8 + c + 1],
                    nrstd[:], ALU.mult, ALU.mult, [XR(c, tt), "sm"] + NR, ["ost"])
                if c == 7:
                    DMA("sp", ov[:, :, cols(tt)], ost, ["ost"], (), chan="o0", fill=tt)

            return lambda t_: norm_tile(4, t_, norm_to_out)

        mlp(ws_mlp1, final_factory)
        return finish()


def _consts():
    half = 32
    inv = (np.float32(10000.0) ** (-np.arange(half, dtype=np.float32) / np.float32(half))).astype(np.float32)
    pos = np.arange(T, dtype=np.float32)
    ang = (pos[:, None] * inv[None, :]).astype(np.float32)
    cos = np.cos(ang.astype(np.float64)).astype(np.float32)
    sin = np.sin(ang.astype(np.float64)).astype(np.float32)
    cst = np.zeros((128, NCST), np.float32)
    pidx = np.arange(128)
    d = pidx % 64
    jj = d % 32
    cst[:, C_COS:C_COS + T] = cos[:, jj].T
    sgn = np.where(d < 32, -1.0, 1.0).astype(np.float32)
    cst[:, C_SIN:C_SIN + T] = sin[:, jj].T * sgn[:, None]
    partner = np.where(d < 32, pidx + 32, pidx - 32)
    cst[partner, C_PM + pidx] = 1.0
    cst[:, C_BO:C_BO + 128] = (pidx[:, None] // 64 == pidx[None, :] // 64) / 64.0
    cst[:, C_ON:C_ON + 128] = 1.0 / 1024.0
    ii = np.arange(128)
    mA = (ii[None, :] <= pidx[:, None]).astype(np.float32)
    mB = (ii[None, :] >= pidx[:, None]).astype(np.float32)
    cst[:, C_M4:C_M4 + 512] = np.concatenate([mA, mB, mA, mB], axis=1)
    cst[:, C_O64:C_O64 + 64] = 1.0
    return cst


def _prep_shared(inp):
    f = lambda a: np.ascontiguousarray(np.asarray(a, dtype=np.float32))
    wqkv = f(inp["l0_w_qkv"])
    wo = f(inp["l0_w_o"])
    wq = np.zeros((9, 1024, 384), np.float32)
    wo9 = np.zeros((9, 128, 1024), np.float32)
    for pi in range(3):
        for g in range(3):
            u = pi * 3 + g
            heads = [5 * g + 2 * pi, 5 * g + 2 * pi + 1] if pi < 2 else [5 * g + 4]
            for s, h in enumerate(heads):
                for part in range(3):
                    wq[u, :, part * 128 + s * 64:part * 128 + (s + 1) * 64] = \
                        wqkv[:, part * 960 + h * 64:part * 960 + (h + 1) * 64]
                wo9[u, s * 64:(s + 1) * 64, :] = wo[h * 64:(h + 1) * 64, :]
    sm = np.zeros((128, NSM), np.float32)
    gn = [inp["l0_norm_mix"], inp["l0_norm_mlp"], inp["l1_norm_mix"], inp["l1_norm_mlp"], inp["final_norm"]]
    for i, gv in enumerate(gn):
        sm[:, S_GN + i * 8:S_GN + (i + 1) * 8] = f(gv).reshape(8, 128).T
    sm[:, S_GQK] = np.tile(f(inp["l0_q_norm"]), 2)
    sm[:, S_GQK + 1] = np.tile(f(inp["l0_k_norm"]), 2)
    cw = f(inp["l1_conv_w"])
    for k in range(3):
        sm[:, S_CW + k * 8:S_CW + (k + 1) * 8] = cw[k].reshape(8, 128).T
    sm[:, S_EPS] = EPS
    win = f(inp["l1_w_in"])
    win8 = np.zeros((8, 1024, 384), np.float32)
    for i in range(8):
        for part in range(3):
            win8[i, :, part * 128:(part + 1) * 128] = win[:, part * 1024 + i * 128:part * 1024 + (i + 1) * 128]
    return {
        "cst": _consts(), "sm": sm, "wqkv": wq, "wo": wo9,
        "wup0": f(inp["l0_w_up"]), "wdn0": f(inp["l0_w_down"]),
        "win": win8, "wout": f(inp["l1_w_out"]),
        "wup1": f(inp["l1_w_up"]), "wdn1": f(inp["l1_w_down"]),
    }


def run(inputs, n_cores=8, stop_after=None):
    x = np.asarray(inputs["x"], dtype=np.float32)
    shared = _prep_shared(inputs)
    nc = build_nc(stop_after=stop_after)
    in_maps = []
    for b in range(n_cores):
        m = dict(shared)
        m["xT"] = np.ascontiguousarray(x[b].T)
        in_maps.append(m)
    res = run_bass_kernel_spmd(nc, in_maps, core_ids=list(range(n_cores)))
    out = np.stack([np.ascontiguousarray(r["outT"].T) for r in res.results], axis=0)
    return out.astype(np.float32)


def kernel(**inputs):
    return run(inputs, n_cores=8)
```

```python
from contextlib import ExitStack

import numpy as np
import concourse.bass as bass
import concourse.mybir as mybir
from concourse.bass_utils import run_bass_kernel_spmd

F32 = mybir.dt.float32
BF16 = mybir.dt.bfloat16
AF = mybir.ActivationFunctionType
ALU = mybir.AluOpType

T = 2048
EPS = 1e-6
DILS = (1, 4, 16)
C_COS, C_SIN, C_PM, C_BO, C_ON, C_M4, C_O64, NCST = 0, 2048, 4096, 4224, 4352, 4480, 4992, 5056
S_GN, S_GQK, S_CW, S_EPS, NSM = 0, 40, 42, 66, 68
AR_UNITS = 33500


class _Op:
    __slots__ = ("eng", "fn", "chan", "fill", "deps", "sig", "sig_idx", "idx")


class Prog:
    ENGS = ("pe", "act", "dve", "pool", "sp")

    def __init__(self):
        self.ops = []
        self.res = {}
        self.chan_count = {}
        self.fill_end = {}
        self.chan_eng = {}
        self.final_chans = []

    def alias(self, new, olds):
        st = self.res.setdefault(new, [None, {}])
        for o in olds:
            so = self.res.get(o)
            if so is None or o == new:
                continue
            if so[0] is not None:
                st[1][("w", o)] = so[0]
            for k, op in list(so[1].items()):
                st[1][(k, o)] = op

    def op(self, eng, fn, reads=(), writes=(), chan=None, fill=None):
        o = _Op()
        o.eng, o.fn, o.chan, o.fill = eng, fn, chan, fill
        o.idx = len(self.ops)
        o.sig = False
        o.sig_idx = 0
        deps = set()
        for r in reads:
            st = self.res.get(r)
            if st is not None and st[0] is not None:
                deps.add(st[0])
            if st is not None and r.startswith("ps") and eng in ("act", "dve"):
                other = "dve" if eng == "act" else "act"
                if other in st[1]:
                    deps.add(st[1][other])
        for w in writes:
            st = self.res.get(w)
            if st is not None:
                if st[0] is not None:
                    deps.add(st[0])
                deps.update(st[1].values())
        key = eng if chan is None else ("dma", chan)
        for r in reads:
            st = self.res.setdefault(r, [None, {}])
            st[1][key] = o
        for w in writes:
            self.res[w] = [o, {}]
        deps.discard(o)
        if chan is not None:
            deps = {d for d in deps if not (d.chan == chan and d.fill == fill)}
        o.deps = deps
        if chan is not None:
            c = self.chan_count.get(chan, 0) + 1
            self.chan_count[chan] = c
            self.fill_end[(chan, fill)] = c
            assert self.chan_eng.setdefault(chan, eng) == eng
        self.ops.append(o)
        return o

    def emit(self, nc, stack):
        for o in self.ops:
            for d in o.deps:
                if d.chan is None and not (d.eng == "pe" and o.eng == "pe"):
                    d.sig = True
        cnt = {e: 0 for e in self.ENGS}
        for o in self.ops:
            if o.chan is None and o.sig:
                cnt[o.eng] += 1
                o.sig_idx = cnt[o.eng]
        sems = {}
        for e in self.ENGS:
            sems[("eng", e)] = stack.enter_context(nc.semaphore("s_" + e))
        for c in self.chan_count:
            sems[("chan", c)] = stack.enter_context(nc.semaphore("c_" + c))
        block = stack.enter_context(nc.Block())
        prog = self

        def run_stream(ename, eng):
            waited = {}
            for o in prog.ops:
                if o.eng != ename:
                    continue
                needs = {}
                for d in o.deps:
                    if d.chan is not None:
                        k = ("chan", d.chan)
                        v = 16 * prog.fill_end[(d.chan, d.fill)]
                    else:
                        if d.eng == ename and ename == "pe":
                            continue
                        k = ("eng", d.eng)
                        v = d.sig_idx
                    if v > needs.get(k, 0):
                        needs[k] = v
                for k, v in needs.items():
                    if waited.get(k, 0) >= v:
                        continue
                    eng.wait_ge(sems[k], v)
                    waited[k] = v
                inst = o.fn(eng)
                if o.chan is not None:
                    inst.then_inc(sems[("chan", o.chan)], 16)
                elif o.sig:
                    inst.then_inc(sems[("eng", ename)], 1)
            for c, n in prog.chan_count.items():
                if prog.chan_eng[c] == ename and waited.get(("chan", c), 0) < 16 * n:
                    eng.wait_ge(sems[("chan", c)], 16 * n)

        block.tensor(lambda e: run_stream("pe", e))
        block.scalar(lambda e: run_stream("act", e))
        block.vector(lambda e: run_stream("dve", e))
        block.gpsimd(lambda e: run_stream("pool", e))
        block.sync(lambda e: run_stream("sp", e))


class _Stop(Exception):
    pass


class _Ring:
    def __init__(self, ids):
        self.ids = list(ids)
        self.i = 0

    def next(self):
        b = self.ids[self.i % len(self.ids)]
        self.i += 1
        return b


def build_nc(stop_after=None):
    nc = bass.Bass("TRN2", target_bir_lowering=False)

    def din(name, shape):
        return nc.dram_tensor(name, list(shape), F32, kind="ExternalInput").ap()

    xT_d = din("xT", [1024, T])
    cst_d = din("cst", [128, NCST])
    sm_d = din("sm", [128, NSM])
    wqkv_d = din("wqkv", [9, 1024, 384])
    wo_d = din("wo", [9, 128, 1024])
    wup_d = [din("wup0", [1024, 4096]), din("wup1", [1024, 4096])]
    wdn_d = [din("wdn0", [4096, 1024]), din("wdn1", [4096, 1024])]
    win_d = din("win", [8, 1024, 384])
    wout_d = din("wout", [1024, 1024])
    out_d = nc.dram_tensor("outT", [1024, T], F32, kind="ExternalOutput").ap()

    with ExitStack() as st:
        def sb(name, shape, dt):
            return st.enter_context(nc.sbuf_tensor(name, list(shape), dt))

        X = sb("X", [128, 8, T], F32)
        hT = sb("hT", [128, 8, T], BF16)
        cst = sb("cstb", [128, NCST], BF16)
        sm = sb("smf", [128, NSM], F32)
        wsl = [sb(f"wsl{i}", [128, 4096], BF16) for i in range(4)]
        nsq = [sb(f"nsq{i}", [128, 512], BF16) for i in range(2)]
        nrs = sb("nrs", [128, 1024], BF16)
        nrstd = nrs[:].bitcast(F32)
        nrb = [nrs[:, 0:512], nrs[:, 512:1024]]
        NR = ["nr0", "nr1"]
        arena = sb("arena", [128, AR_UNITS], BF16)
        ps = [st.enter_context(nc.psum_tensor(f"ps{i}", [128, 512], F32)) for i in range(8)]

        p = Prog()

        def MM(out, lhsT, rhs, start, stop, R, W):
            p.op("pe", lambda e: e.matmul(out, lhsT=lhsT, rhs=rhs, start=start, stop=stop), R, W)

        def ACT(out, in_, func, R, W, scale=None, bias=None):
            kw = {}
            if scale is not None:
                kw["scale"] = scale
            if bias is not None:
                kw["bias"] = bias
            p.op("act", lambda e: e.activation(out=out, in_=in_, func=func, **kw), R, W)

        def TT(eng, out, in0, in1, op, R, W):
            p.op(eng, lambda e: e.tensor_tensor(out=out, in0=in0, in1=in1, op=op), R, W)

        def TS(eng, out, in0, s1, op0, R, W):
            p.op(eng, lambda e: e.tensor_scalar(out=out, in0=in0, scalar1=s1, scalar2=None, op0=op0), R, W)

        def STT(out, in0, scalar, in1, op0, op1, R, W):
            p.op("dve", lambda e: e.scalar_tensor_tensor(out=out, in0=in0, scalar=scalar, in1=in1,
                                                         op0=op0, op1=op1), R, W)

        def CP(eng, out, in_, R, W):
            if eng == "act":
                p.op("act", lambda e: e.copy(out=out, in_=in_), R, W)
            else:
                p.op(eng, lambda e: e.tensor_copy(out=out, in_=in_), R, W)

        def MEMSET(eng, ap, val, W):
            p.op(eng, lambda e: e.memset(ap, val), (), W)

        def DMA(eng, out, in_, R, W, chan, fill):
            p.op(eng, lambda e: e.dma_start(out=out, in_=in_), R, W, chan=chan, fill=fill)

        arena_bufs = []

        def aalloc(phase, units, res_names, cursor):
            s = cursor[0]
            e = s + units + (units & 1)
            assert e <= AR_UNITS, (phase, e)
            cursor[0] = e
            for (ph, s0, e0, names) in arena_bufs:
                if ph != phase and s0 < e and s < e0:
                    for rn in res_names:
                        p.alias(rn, names)
            arena_bufs.append((phase, s, e, list(res_names)))
            return arena[:, s:s + units]

        def as_f32(ap):
            return ap.bitcast(F32)

        cosT = cst[:, C_COS:C_COS + T]
        sinS = cst[:, C_SIN:C_SIN + T]
        Pm = cst[:, C_PM:C_PM + 128]
        blockones = cst[:, C_BO:C_BO + 128]
        onesN = cst[:, C_ON:C_ON + 128]
        mask4 = cst[:, C_M4:C_M4 + 512].rearrange("p (s i) -> p s i", s=4)
        ones64 = cst[:, C_O64:C_O64 + 64]
        eps_ap = sm[:, S_EPS:S_EPS + 1]

        pn = _Ring([3, 4])
        rings = {"pp": _Ring([0, 1, 2, 5, 6, 7])}

        def PSB(b):
            return "ps%d" % b

        wsteps = []

        def add_wstep(fn):
            wsteps.append(fn)
            return len(wsteps) - 1

        wstate = {"next": 0}

        def prefetch(upto):
            while wstate["next"] <= min(upto, len(wsteps) - 1):
                k = wstate["next"]
                slot = k % 4
                gate = ["h7_0"] if 1 <= k <= 3 else ()
                for (o_ap, i_ap) in wsteps[k](wsl[slot]):
                    DMA("pool", o_ap, i_ap, gate, ["w%d" % slot], chan="w%d" % slot, fill=k)
                wstate["next"] += 1

        def wslot(k, pf_to=None):
            prefetch(k + 3 if pf_to is None else pf_to)
            return k % 4, wsl[k % 4], "w%d" % (k % 4)

        def mk_rows(dview, kc, cols):
            def fn(tile):
                o = tile[:, 0:kc * cols].rearrange("p (c f) -> p c f", c=kc)
                return [(o, dview.rearrange("(c p) f -> p c f", p=128))]
            return fn

        ws_qkv = [add_wstep(mk_rows(wqkv_d[u], 8, 384)) for u in range(9)]

        def mk_wo(k):
            def fn(tile):
                o = tile[:, 0:3 * 1024].rearrange("p (c f) -> p c f", c=3)
                return [(o, wo_d[3 * k:3 * k + 3].rearrange("u p f -> p u f"))]
            return fn

        ws_wo = [add_wstep(mk_wo(k)) for k in range(3)]

        def mlp_steps(l):
            r = []
            for fg in range(8):
                a = add_wstep(mk_rows(wup_d[l][:, fg * 512:(fg + 1) * 512], 8, 512))
                b = add_wstep(mk_rows(wdn_d[l][fg * 512:(fg + 1) * 512, :], 4, 1024))
                r.append((a, b))
            return r

        ws_mlp0 = mlp_steps(0)
        ws_win = [add_wstep(mk_rows(win_d[i], 8, 384)) for i in range(8)]
        ws_wout = [add_wstep(mk_rows(wout_d[k * 512:(k + 1) * 512, :], 4, 1024)) for k in range(2)]
        ws_mlp1 = mlp_steps(1)

        DMA("sp", sm[:], sm_d, (), ["sm"], chan="sm", fill=0)
        dbg = stop_after or ""
        for j in range((NCST + 1023) // 1024):
            if dbg.startswith("load") and "c" not in dbg[4:]:
                break
            c0, c1 = j * 1024, min(NCST, (j + 1) * 1024)
            DMA("pool", cst[:, c0:c1], cst_d[:, c0:c1], (), ["cst"], chan="cst", fill=0)
        xv = xT_d.rearrange("(c p) t -> p c t", p=128)
        for tt in range(4):
            for c in range(8):
                DMA("sp", X[:, c, tt * 512:(tt + 1) * 512], xv[:, c, tt * 512:(tt + 1) * 512],
                    ["xg%d" % (tt - 1)] if tt else (), ["X%d_%d" % (c, tt), "xg%d" % tt],
                    chan="x%d" % tt, fill=0)
        if not (dbg.startswith("load") and "w" not in dbg[4:]):
            prefetch(0)

        def XR(c, tt):
            return "X%d_%d" % (c, tt)

        def HR(c, tt):
            return "h%d_%d" % (c, tt)

        def cols(tt):
            return slice(tt * 512, (tt + 1) * 512)

        def norm_stage(gidx, out_fn):
            for tt in range(4):
                norm_tile(gidx, tt, out_fn)

        def norm_tile(gidx, tt, out_fn):
            if True:
                b = pn.next()
                for c in range(8):
                    sq = nsq[c % 2]
                    ACT(sq[:], X[:, c, cols(tt)], AF.Square, [XR(c, tt)], ["nsq%d" % (c % 2)])
                    MM(ps[b][:], onesN, sq[:], c == 0, c == 7, ["cst", "nsq%d" % (c % 2)], [PSB(b)])
                ACT(nrstd[:], ps[b][:], AF.Ln, [PSB(b), "sm"], NR, bias=eps_ap)
                ACT(nrstd[:], nrstd[:], AF.Exp, NR, NR, scale=-0.5)
                for c in range(8):
                    out_fn(gidx, c, tt)

        def tail_norm(tt, fn):
            if fn is None:
                return
            if tt >= 1:
                fn(tt - 1)
            if tt == 3:
                fn(3)

        def norm_to_hT(gidx, c, tt):
            STT(hT[:, c, cols(tt)], X[:, c, cols(tt)], sm[:, S_GN + gidx * 8 + c:S_GN + gidx * 8 + c + 1],
                nrstd[:], ALU.mult, ALU.mult, [XR(c, tt), "sm"] + NR, [HR(c, tt)])

        def dump_X():
            ov = out_d.rearrange("(c p) t -> p c t", p=128)
            for c in range(8):
                DMA("sp", ov[:, c, :], X[:, c, :], [XR(c, tt) for tt in range(4)], (), chan="o0", fill=0)
            p.final_chans.append("o0")

        def finish():
            p.emit(nc, st)
            return nc

        if dbg.startswith("load"):
            dump_X()
            return finish()
        norm_stage(0, norm_to_hT)
        if stop_after == "norm0":
            dump_X()
            return finish()

        cur = [0]
        qT = aalloc("A", 2048, ["qT"], cur)
        kT = aalloc("A", 2048, ["kT"], cur)
        Vt = aalloc("A", 2048, ["Vt"], cur).rearrange("p (k f) -> p k f", k=16)
        oT = [aalloc("A", 2048, ["oT%d" % u], cur) for u in range(9)]
        dtot = as_f32(aalloc("A", 4096, ["dtot"], cur))
        qg = [aalloc("A", 512, ["qg%d" % i], cur) for i in range(2)]
        T1 = [aalloc("A", 512, ["t1_%d" % i], cur) for i in range(2)]
        T2 = [aalloc("A", 512, ["t2_%d" % i], cur) for i in range(2)]
        Pb = [aalloc("A", 512, ["P%d" % i], cur).rearrange("p (s i) -> p s i", s=4) for i in range(3)]

        for u_ in (6, 7, 8):
            MEMSET("pool", oT[u_][64:128, :], 0.0, ["oT%d" % u_])
        rings["pp"] = _Ring([0, 1])
        pSA = _Ring([0, 1])
        pSB = _Ring([3, 4])
        pOo = _Ring([2, 5])
        pOd = _Ring([6, 7])
        qk_i = [0]
        p_i = [0]

        def attn_unit(u, pi, g):
            D = DILS[g]
            L = T // D
            nb = L // 128
            slot, wt, wres = wslot(ws_qkv[u])
            W = wt[:, 0:8 * 384].rearrange("p (c f) -> p c f", c=8)
            pp = rings["pp"]
            its = [(which, dst, dres, coff, tt) for which, (dst, dres, coff) in
                   enumerate([(qT, "qT", 0), (kT, "kT", 128)]) for tt in range(4)]

            def proj(it):
                which, dst, dres, coff, tt = it
                b = pp.next()
                for kc in range(8):
                    MM(ps[b][:], W[:, kc, coff:coff + 128], hT[:, kc, cols(tt)], kc == 0, kc == 7,
                       [wres, HR(kc, tt)], [PSB(b)])
                return b

            nxt = proj(its[0])
            for k_, it in enumerate(its):
                if True:
                    which, dst, dres, coff, tt = it
                    b = nxt
                    nxt = proj(its[k_ + 1]) if k_ + 1 < len(its) else None
                    i = qk_i[0]
                    qk_i[0] += 1
                    sq, sqr = nsq[i % 2], "nsq%d" % (i % 2)
                    qgb, qgr = qg[i % 2], "qg%d" % (i % 2)
                    ACT(sq[:], ps[b][:], AF.Square, [PSB(b)], [sqr])
                    ACT(qgb, ps[b][:], AF.Copy, [PSB(b), "sm"], [qgr],
                        scale=sm[:, S_GQK + which:S_GQK + which + 1])
                    bs = pn.next()
                    MM(ps[bs][:], blockones, sq[:], True, True, ["cst", sqr], [PSB(bs)])
                    bw = pn.next()
                    MM(ps[bw][:], Pm, qgb, True, True, ["cst", qgr], [PSB(bw)])
                    j = i % 2
                    ACT(ps[bs][:], ps[bs][:], AF.Ln, [PSB(bs), "sm"], [PSB(bs)], bias=eps_ap)
                    ACT(nrb[j], ps[bs][:], AF.Exp, [PSB(bs)], ["nr%d" % j], scale=-0.5)
                    t1, t1r = T1[j], "t1_%d" % j
                    t2, t2r = T2[j], "t2_%d" % j
                    TT("pool", t1, qgb, cosT[:, cols(tt)], ALU.mult, [qgr, "cst"], [t1r])
                    TT("dve", t2, ps[bw][:], sinS[:, cols(tt)], ALU.mult, [PSB(bw), "cst"], [t2r])
                    TT("dve", t1, t1, t2, ALU.add, [t1r, t2r], [t1r])
                    w = 512 // D
                    dv = dst.rearrange("p (r j) -> p r j", r=D)[:, :, tt * w:(tt + 1) * w]
                    s1 = t1.rearrange("p (j r) -> p r j", r=D)
                    s2 = nrb[j].rearrange("p (j r) -> p r j", r=D)
                    TT("dve", dv, s1, s2, ALU.mult, [t1r, "nr%d" % j], [dres])
            if dbg == "u%dqk" % u:
                CP("dve", X[:, 0, :], qT, ["qT"], [XR(0, t_) for t_ in range(4)])
                CP("dve", X[:, 1, :], kT, ["kT"], [XR(1, t_) for t_ in range(4)])
                raise _Stop()
            for k4 in range(4):
                b = pp.next()
                for q4 in range(4):
                    kt = k4 * 4 + q4
                    r, bb = kt // nb, kt % nb
                    t0 = 128 * bb * D + r
                    tts = sorted(set([t0 // 512, (t0 + 127 * D) // 512]))
                    if D == 16:
                        tts = [0, 1, 2, 3]
                    for kc in range(8):
                        lhsT = hT[:, kc, t0:t0 + 127 * D + 1:D] if D > 1 else hT[:, kc, t0:t0 + 128]
                        vw = 64 if pi == 2 else 128
                        MM(ps[b][:, q4 * 128:q4 * 128 + vw], lhsT, W[:, kc, 256:256 + vw], kc == 0, kc == 7,
                           [wres] + [HR(kc, x) for x in tts], [PSB(b)])
                CP("act", Vt[:, k4 * 4:(k4 + 1) * 4, :], ps[b][:].rearrange("p (k f) -> p k f", k=4),
                   [PSB(b)], ["Vt"])
            if dbg == "u%dv" % u:
                CP("dve", X[:, 2, :], Vt.rearrange("p k f -> p (k f)"), ["Vt"], [XR(2, t_) for t_ in range(4)])
                raise _Stop()
            blocks = [(r, a) for r in range(D) for a in range(nb + 1)]
            heads = (0,) if pi == 2 else (0, 1)
            nh = len(heads)

            def geom(blk):
                r, a = blk
                i0, i1 = (64, 128) if a == 0 else ((0, 64) if a == nb else (0, 128))
                tiles = [bt for bt in (a - 1, a) if 0 <= bt < nb]
                return r, a, i0, i1, tiles

            def slot_of(h, bt, a):
                return 2 * h + (0 if bt == a - 1 else 1)

            def emit_S(blk):
                r, a, i0, i1, tiles = geom(blk)
                nq = i1 - i0
                jlo = 128 * a - 64 + i0
                banks = (pSA.next(), pSB.next() if nh == 2 else None)
                for h in heads:
                    for bt in tiles:
                        s = 0 if bt == a - 1 else 1
                        MM(ps[banks[h]][:, s * 128:s * 128 + nq],
                           kT[h * 64:(h + 1) * 64, r * L + 128 * bt:r * L + 128 * bt + 128],
                           qT[h * 64:(h + 1) * 64, r * L + jlo:r * L + jlo + nq], True, True,
                           ["qT", "kT"], [PSB(banks[h])])
                return banks

            def emit_P(blk, banks):
                r, a, i0, i1, tiles = geom(blk)
                nq = i1 - i0
                pi_ = p_i[0] % 3
                p_i[0] += 1
                P, pr = Pb[pi_], "P%d" % pi_
                for h in heads:
                    psv = ps[banks[h]][:, 0:256].rearrange("p (s i) -> p s i", s=2)
                    if len(tiles) == 2:
                        ACT(P[:, 2 * h:2 * h + 2, :], psv, AF.Exp, [PSB(banks[h])], [pr], scale=0.125)
                    else:
                        s0 = 1 if a == 0 else 0
                        ACT(P[:, 2 * h + s0, 0:nq], psv[:, s0, 0:nq], AF.Exp, [PSB(banks[h])], [pr], scale=0.125)
                if len(tiles) == 2:
                    pv, mv = P[:, 0:2 * nh, :], mask4[:, 0:2 * nh, :]
                else:
                    s0 = 1 if a == 0 else 0
                    pv = P[:, s0:2 * nh:2, 0:nq]
                    mv = mask4[:, s0:2 * nh:2, i0:i1]
                TT("dve", pv, pv, mv, ALU.mult, [pr, "cst"], [pr])
                return P, pr

            def emit_O(blk, P, pr):
                r, a, i0, i1, tiles = geom(blk)
                nq = i1 - i0
                jlo = 128 * a - 64 + i0
                bo, bd = pOo.next(), pOd.next()
                for kind, bank in ((0, bo), (1, bd)):
                    for h in heads:
                        hp = slice(h * 64, (h + 1) * 64)
                        for ti, bt in enumerate(tiles):
                            s = slot_of(h, bt, a)
                            kt = r * nb + bt
                            lhsT = Vt[:, kt, hp] if kind == 0 else ones64
                            MM(ps[bank][hp, 0:nq], lhsT, P[:, s, 0:nq],
                               ti == 0, ti == len(tiles) - 1,
                               [pr, "Vt" if kind == 0 else "cst"], [PSB(bank)])
                t0 = jlo * D + r
                tok = slice(t0, t0 + (nq - 1) * D + 1, D) if D > 1 else slice(t0, t0 + nq)
                rows = slice(0, 64 * nh)
                CP("act", oT[u][rows, tok], ps[bo][rows, 0:nq], [PSB(bo)], ["oT%d" % u])
                if g == 0:
                    CP("dve", dtot[rows, tok], ps[bd][rows, 0:nq], [PSB(bd)], ["dtot"])
                else:
                    TT("dve", dtot[rows, tok], dtot[rows, tok], ps[bd][rows, 0:nq], ALU.add,
                       [PSB(bd), "dtot"], ["dtot"])

            n = len(blocks)
            sb_ = {0: emit_S(blocks[0])}
            if n > 1:
                sb_[1] = emit_S(blocks[1])
            Pcur = emit_P(blocks[0], sb_[0])
            for bi in range(n):
                if bi + 2 < n:
                    sb_[bi + 2] = emit_S(blocks[bi + 2])
                Pnext = emit_P(blocks[bi + 1], sb_[bi + 1]) if bi + 1 < n else None
                emit_O(blocks[bi], *Pcur)
                Pcur = Pnext

        for pi in range(3):
            for g in range(3):
                try:
                    attn_unit(pi * 3 + g, pi, g)
                except _Stop:
                    dump_X()
                    return finish()
                if dbg == "u%datt" % (pi * 3 + g):
                    CP("dve", X[:, 3, :], oT[pi * 3 + g], ["oT%d" % (pi * 3 + g)], [XR(3, t_) for t_ in range(4)])
                    CP("dve", X[:, 4, :], dtot, ["dtot"], [XR(4, t_) for t_ in range(4)])
                    dump_X()
                    return finish()
            ACT(dtot, dtot, AF.Ln, ["dtot"], ["dtot"])
            ACT(dtot, dtot, AF.Exp, ["dtot"], ["dtot"], scale=-1.0)
            for g in range(3):
                u = pi * 3 + g
                TT("pool", oT[u], oT[u], dtot, ALU.mult, ["oT%d" % u, "dtot"], ["oT%d" % u])

        rings["pp"] = _Ring([0, 1, 2, 5, 6, 7])
        wo_slots = [wslot(k, pf_to=ws_wo[0] + 3) for k in ws_wo]
        for tt in range(4):
            for oc in range(8):
                b = rings["pp"].next()
                for u in range(9):
                    _, wt, wres = wo_slots[u // 3]
                    Wv = wt[:, 0:3 * 1024].rearrange("p (c f) -> p c f", c=3)
                    MM(ps[b][:], Wv[:, u % 3, oc * 128:(oc + 1) * 128], oT[u][:, cols(tt)], u == 0, u == 8,
                       [wres, "oT%d" % u], [PSB(b)])
                TT("dve", X[:, oc, cols(tt)], X[:, oc, cols(tt)], ps[b][:], ALU.add,
                   [XR(oc, tt), PSB(b)], [XR(oc, tt)])
            tail_norm(tt, lambda t_: norm_tile(1, t_, norm_to_hT))
        if stop_after == "attn":
            dump_X()
            return finish()

        mlp_count = [0]

        def mlp(steps, tail_fn_factory):
            ph = "M%d" % mlp_count[0]
            mlp_count[0] += 1
            cur = [0]
            aTb = [aalloc(ph, 8192, ["a%d_%d_%d" % (bf, fc, tt) for fc in range(4) for tt in range(4)], cur)
                   .rearrange("p (c t) -> p c t", c=4) for bf in range(2)]
            rb = [aalloc(ph, 512, ["r%d" % i], cur) for i in range(3)]
            tail_fn = tail_fn_factory(ph, cur)
            pp = rings["pp"]
            ri = 0
            for fg in range(8):
                su, sd = steps[fg]
                _, wtu, wru = wslot(su)
                _, wtd, wrd = wslot(sd, pf_to=su + 3)
                WU = wtu[:, 0:8 * 512].rearrange("p (c f) -> p c f", c=8)
                WD = wtd[:, 0:4 * 1024].rearrange("p (c f) -> p c f", c=4)
                bf = fg % 2
                aT = aTb[bf]
                for tt in range(4):
                    for fc in range(4):
                        b = pp.next()
                        for kc in range(8):
                            MM(ps[b][:], WU[:, kc, fc * 128:(fc + 1) * 128], hT[:, kc, cols(tt)],
                               kc == 0, kc == 7, [wru, HR(kc, tt)], [PSB(b)])
                        r, rr = rb[ri % 3], "r%d" % (ri % 3)
                        ri += 1
                        ACT(r, ps[b][:], AF.Relu, [PSB(b)], [rr])
                        TT("pool", aT[:, fc, cols(tt)], r, r, ALU.mult, [rr], ["a%d_%d_%d" % (bf, fc, tt)])
                for tt in range(4):
                    for oc in range(8):
                        b = pp.next()
                        for fc in range(4):
                            MM(ps[b][:], WD[:, fc, oc * 128:(oc + 1) * 128], aT[:, fc, cols(tt)],
                               fc == 0, fc == 3, [wrd, "a%d_%d_%d" % (bf, fc, tt)], [PSB(b)])
                        TT("dve", X[:, oc, cols(tt)], X[:, oc, cols(tt)], ps[b][:], ALU.add,
                           [XR(oc, tt), PSB(b)], [XR(oc, tt)])
                    if fg == 7:
                        tail_norm(tt, tail_fn)

        mlp(ws_mlp0, lambda ph, cur: (lambda t_: norm_tile(2, t_, norm_to_hT)))
        if stop_after == "mlp0":
            dump_X()
            return finish()

        cur = [0]
        zT = aalloc("C", 8 * T, ["z%d" % i for i in range(8)], cur).rearrange("p (c t) -> p c t", c=8)
        vb = [as_f32(aalloc("C", 2 * (T + 2), ["v0"], cur))] * 2
        yb = as_f32(aalloc("C", 2 * T, ["y"], cur))
        bsb = [aalloc("C", T, ["b%d" % i], cur) for i in range(2)]
        csb = [as_f32(aalloc("C", 1024, ["c%d" % i], cur)) for i in range(2)]
        MEMSET("pool", vb[0][:, 0:1], 0.0, ["v0"])
        MEMSET("pool", vb[0][:, T + 1:T + 2], 0.0, ["v0"])
        pp = rings["pp"]
        ci = 0
        for i in range(8):
            _, wt, wres = wslot(ws_win[i])
            W = wt[:, 0:8 * 384].rearrange("p (c f) -> p c f", c=8)
            v, vr = vb[0], "v0"
            bs_, br = bsb[i % 2], "b%d" % (i % 2)
            for tt in range(4):
                banks = []
                for coff in (128, 256, 0):
                    b = pp.next()
                    banks.append(b)
                    for kc in range(8):
                        MM(ps[b][:], W[:, kc, coff:coff + 128], hT[:, kc, cols(tt)], kc == 0, kc == 7,
                           [wres, HR(kc, tt)], [PSB(b)])
                cs, cr = csb[ci % 2], "c%d" % (ci % 2)
                ci += 1
                CP("act", cs, ps[banks[0]][:], [PSB(banks[0])], [cr])
                TT("dve", v[:, 1 + tt * 512:1 + (tt + 1) * 512], cs, ps[banks[1]][:], ALU.mult,
                   [cr, PSB(banks[1])], [vr])
                CP("act", bs_[:, cols(tt)], ps[banks[2]][:], [PSB(banks[2])], [br])
            cw = lambda k: sm[:, S_CW + k * 8 + i:S_CW + k * 8 + i + 1]
            ACT(yb, v[:, 0:T], AF.Copy, [vr, "sm"], ["y"], scale=cw(0))
            STT(yb, v[:, 1:T + 1], cw(1), yb, ALU.mult, ALU.add, [vr, "sm", "y"], ["y"])
            STT(yb, v[:, 2:T + 2], cw(2), yb, ALU.mult, ALU.add, [vr, "sm", "y"], ["y"])
            TT("pool", zT[:, i, :], bs_, yb, ALU.mult, [br, "y"], ["z%d" % i])
        wout_slots = [wslot(k, pf_to=ws_wout[0] + 3) for k in ws_wout]
        for tt in range(4):
            for oc in range(8):
                b = pp.next()
                for kc in range(8):
                    _, wt, wres = wout_slots[kc // 4]
                    Wv = wt[:, 0:4 * 1024].rearrange("p (c f) -> p c f", c=4)
                    MM(ps[b][:], Wv[:, kc % 4, oc * 128:(oc + 1) * 128], zT[:, kc, cols(tt)], kc == 0, kc == 7,
                       [wres, "z%d" % kc], [PSB(b)])
                TT("dve", X[:, oc, cols(tt)], X[:, oc, cols(tt)], ps[b][:], ALU.add,
                   [XR(oc, tt), PSB(b)], [XR(oc, tt)])
            tail_norm(tt, lambda t_: norm_tile(3, t_, norm_to_hT))
        if stop_after == "conv":
            dump_X()
            return finish()

        ov = out_d.rearrange("(c p) t -> p c t", p=128)

        def final_factory(ph, cur):
            ost = as_f32(aalloc(ph, 8192, ["ost"], cur)).rearrange("p (c t) -> p c t", c=8)

            def norm_to_out(gidx, c, tt):
                STT(ost[:, c, :], X[:, c, cols(tt)], sm[:, S_GN + gidx * 8 + c:S_GN + gidx * 8 + c + 1],
                    nrstd[:], ALU.mult, ALU.mult, [XR(c, tt), "sm"] + NR, ["ost"])
                if c == 7:
                    DMA("sp", ov[:, :, cols(tt)], ost, ["ost"], (), chan="o0", fill=tt)

            return lambda t_: norm_tile(4, t_, norm_to_out)

        mlp(ws_mlp1, final_factory)
        return finish()


def _consts():
    half = 32
    inv = (np.float32(10000.0) ** (-np.arange(half, dtype=np.float32) / np.float32(half))).astype(np.float32)
    pos = np.arange(T, dtype=np.float32)
    ang = (pos[:, None] * inv[None, :]).astype(np.float32)
    cos = np.cos(ang.astype(np.float64)).astype(np.float32)
    sin = np.sin(ang.astype(np.float64)).astype(np.float32)
    cst = np.zeros((128, NCST), np.float32)
    pidx = np.arange(128)
    d = pidx % 64
    jj = d % 32
    cst[:, C_COS:C_COS + T] = cos[:, jj].T
    sgn = np.where(d < 32, -1.0, 1.0).astype(np.float32)
    cst[:, C_SIN:C_SIN + T] = sin[:, jj].T * sgn[:, None]
    partner = np.where(d < 32, pidx + 32, pidx - 32)
    cst[partner, C_PM + pidx] = 1.0
    cst[:, C_BO:C_BO + 128] = (pidx[:, None] // 64 == pidx[None, :] // 64) / 64.0
    cst[:, C_ON:C_ON + 128] = 1.0 / 1024.0
    ii = np.arange(128)
    mA = (ii[None, :] <= pidx[:, None]).astype(np.float32)
    mB = (ii[None, :] >= pidx[:, None]).astype(np.float32)
    cst[:, C_M4:C_M4 + 512] = np.concatenate([mA, mB, mA, mB], axis=1)
    cst[:, C_O64:C_O64 + 64] = 1.0
    return cst


def _prep_shared(inp):
    f = lambda a: np.ascontiguousarray(np.asarray(a, dtype=np.float32))
    wqkv = f(inp["l0_w_qkv"])
    wo = f(inp["l0_w_o"])
    wq = np.zeros((9, 1024, 384), np.float32)
    wo9 = np.zeros((9, 128, 1024), np.float32)
    for pi in range(3):
        for g in range(3):
            u = pi * 3 + g
            heads = [5 * g + 2 * pi, 5 * g + 2 * pi + 1] if pi < 2 else [5 * g + 4]
            for s, h in enumerate(heads):
                for part in range(3):
                    wq[u, :, part * 128 + s * 64:part * 128 + (s + 1) * 64] = \
                        wqkv[:, part * 960 + h * 64:part * 960 + (h + 1) * 64]
                wo9[u, s * 64:(s + 1) * 64, :] = wo[h * 64:(h + 1) * 64, :]
    sm = np.zeros((128, NSM), np.float32)
    gn = [inp["l0_norm_mix"], inp["l0_norm_mlp"], inp["l1_norm_mix"], inp["l1_norm_mlp"], inp["final_norm"]]
    for i, gv in enumerate(gn):
        sm[:, S_GN + i * 8:S_GN + (i + 1) * 8] = f(gv).reshape(8, 128).T
    sm[:, S_GQK] = np.tile(f(inp["l0_q_norm"]), 2)
    sm[:, S_GQK + 1] = np.tile(f(inp["l0_k_norm"]), 2)
    cw = f(inp["l1_conv_w"])
    for k in range(3):
        sm[:, S_CW + k * 8:S_CW + (k + 1) * 8] = cw[k].reshape(8, 128).T
    sm[:, S_EPS] = EPS
    win = f(inp["l1_w_in"])
    win8 = np.zeros((8, 1024, 384), np.float32)
    for i in range(8):
        for part in range(3):
            win8[i, :, part * 128:(part + 1) * 128] = win[:, part * 1024 + i * 128:part * 1024 + (i + 1) * 128]
    return {
        "cst": _consts(), "sm": sm, "wqkv": wq, "wo": wo9,
        "wup0": f(inp["l0_w_up"]), "wdn0": f(inp["l0_w_down"]),
        "win": win8, "wout": f(inp["l1_w_out"]),
        "wup1": f(inp["l1_w_up"]), "wdn1": f(inp["l1_w_down"]),
    }


def run(inputs, n_cores=8, stop_after=None):
    x = np.asarray(inputs["x"], dtype=np.float32)
    shared = _prep_shared(inputs)
    nc = build_nc(stop_after=stop_after)
    in_maps = []
    for b in range(n_cores):
        m = dict(shared)
        m["xT"] = np.ascontiguousarray(x[b].T)
        in_maps.append(m)
    res = run_bass_kernel_spmd(nc, in_maps, core_ids=list(range(n_cores)))
    out = np.stack([np.ascontiguousarray(r["outT"].T) for r in res.results], axis=0)
    return out.astype(np.float32)


def kernel(**inputs):
    return run(inputs, n_cores=8)
```
